# Optimizing a Trainium2 kernel written in Bass

```python
import jax, jax.numpy as jnp
from jax import lax
import numpy as np

D_MODEL = 2048
BATCH = 1
SEQ = 8192
DEPTH = 2

HEAD_DIM = 128
SB_HEADS = 6
SB_BLOCK = 128
DIL_GROUPS = ((128, 1), (512, 4), (2048, 16))
DIL_HEADS_PER_GROUP = 2
DIL_HEADS = DIL_HEADS_PER_GROUP * len(DIL_GROUPS)
GLA_HEADS = 4
GLA_DK = 128
GLA_DV = 256
GLA_GATE_RANK = 16
GLA_GATE_TAU = 16.0
GLA_CHUNK = 64
N_BRANCH = 3
D_FF = -(-8 * D_MODEL // (3 * 256)) * 256
N_MOD = 6
EPS = 1e-6

SB_W = SB_HEADS * HEAD_DIM
DIL_W = DIL_HEADS * HEAD_DIM
DIL_OUT_W = DIL_HEADS_PER_GROUP * HEAD_DIM
GLA_QK_W = GLA_HEADS * GLA_DK
GLA_V_W = GLA_HEADS * GLA_DV
IN_SPLITS = (SB_W, SB_W, SB_W, DIL_W, DIL_W, DIL_W, GLA_QK_W, GLA_QK_W, GLA_V_W, GLA_V_W, GLA_GATE_RANK, N_BRANCH * D_MODEL)
IN_WIDTH = sum(IN_SPLITS)

kernel_name = "hybrid_sb_dilated_gla_adaln_block"


def rms_norm(x, gain):
    xf = x.astype(jnp.float32)
    y = xf * lax.rsqrt(jnp.mean(jnp.square(xf), axis=-1, keepdims=True) + EPS)
    return (y * gain.astype(jnp.float32)).astype(x.dtype)


def split_heads(x, n):
    b, t, w = x.shape
    return x.reshape(b, t, n, w // n).transpose(0, 2, 1, 3)


def merge_heads(x):
    b, h, t, d = x.shape
    return x.transpose(0, 2, 1, 3).reshape(b, t, h * d)


def alibi_slopes(n):
    return 2.0 ** (-8.0 * jnp.arange(1, n + 1, dtype=jnp.float32) / n)


def stick_breaking_attention(q, k, v):
    b, h, t, d = q.shape
    scale = d ** -0.5
    outs = []
    for blk in range(t // SB_BLOCK):
        start = blk * SB_BLOCK
        end = start + SB_BLOCK
        qb = q[:, :, start:end]
        kb = k[:, :, :end]
        vb = v[:, :, :end]
        z = jnp.einsum('bhqd,bhkd->bhqk', qb, kb).astype(jnp.float32) * scale
        t_pos = start + jnp.arange(SB_BLOCK)[:, None]
        s_pos = jnp.arange(end)[None, :]
        causal = s_pos < t_pos
        log_beta = jax.nn.log_sigmoid(z)
        log_rest = jnp.where(causal, jax.nn.log_sigmoid(-z), 0.0)
        after = lax.cumsum(log_rest, axis=3, reverse=True) - log_rest
        w = jnp.where(causal, jnp.exp(log_beta + after), 0.0)
        outs.append(jnp.einsum('bhqk,bhkd->bhqd', w.astype(v.dtype), vb))
    return jnp.concatenate(outs, axis=2)


def dilated_group_attention(q, k, v, window, dilation, slopes):
    b, h, t, d = q.shape
    blk = window // dilation
    u_len = t // dilation
    nb = -(-u_len // blk)
    u_pad = nb * blk

    def strided(x):
        x = x.reshape(b, h, u_len, dilation, d).transpose(0, 1, 3, 2, 4)
        x = jnp.pad(x, ((0, 0), (0, 0), (0, 0), (0, u_pad - u_len), (0, 0)))
        return x.reshape(b, h, dilation, nb, blk, d)

    def with_prev(x):
        prev = jnp.pad(x[:, :, :, :-1], ((0, 0), (0, 0), (0, 0), (1, 0), (0, 0), (0, 0)))
        return jnp.concatenate([prev, x], axis=4)

    qs = strided(q)
    kk = with_prev(strided(k))
    vv = with_prev(strided(v))
    s = jnp.einsum('bhrnqd,bhrnkd->bhrnqk', qs, kk).astype(jnp.float32) * (d ** -0.5)
    i = jnp.arange(blk)[:, None]
    j = jnp.arange(2 * blk)[None, :]
    delta = blk + i - j
    n = jnp.arange(nb)[:, None, None]
    valid = (delta >= 0) & (delta <= blk) & (n * blk - blk + j >= 0)
    bias = -(slopes.astype(jnp.float32)[:, None, None] * (dilation * delta).astype(jnp.float32))
    logits = jnp.where(valid, s + bias[None, :, None, None], -jnp.inf)
    lse = jax.nn.logsumexp(logits, axis=-1)
    p = jnp.exp(logits - lse[..., None])
    o = jnp.einsum('bhrnqk,bhrnkd->bhrnqd', p.astype(v.dtype), vv)
    o = o.reshape(b, h, dilation, u_pad, d)[:, :, :, :u_len].transpose(0, 1, 3, 2, 4).reshape(b, h, t, d)
    lse = lse.reshape(b, h, dilation, u_pad)[..., :u_len].transpose(0, 1, 3, 2).reshape(b, h, t)
    return o, lse


def dilated_mixture(q, k, v, slopes):
    outs, lses = [], []
    for g, (window, dilation) in enumerate(DIL_GROUPS):
        sl = slice(g * DIL_HEADS_PER_GROUP, (g + 1) * DIL_HEADS_PER_GROUP)
        o, l = dilated_group_attention(q[:, sl], k[:, sl], v[:, sl], window, dilation, slopes[sl])
        outs.append(o)
        lses.append(l)
    weights = jax.nn.softmax(jnp.stack(lses, axis=0), axis=0)
    mixed = jnp.sum(weights[..., None] * jnp.stack(outs, axis=0).astype(jnp.float32), axis=0)
    return mixed.astype(q.dtype)


def gla_chunked(q, k, v, log_a):
    b, h, t, dk = q.shape
    dv = v.shape[-1]
    c = GLA_CHUNK
    n = t // c

    def chunks(x):
        return x.reshape(b, h, n, c, x.shape[-1]).transpose(2, 0, 1, 3, 4)

    qc = chunks((q.astype(jnp.float32) * (dk ** -0.5)))
    kc = chunks(k.astype(jnp.float32))
    vc = chunks(v.astype(jnp.float32))
    ac = chunks(log_a.astype(jnp.float32))
    mask = jnp.tril(jnp.ones((c, c), dtype=bool))

    def step(state, inp):
        qi, ki, vi, ai = inp
        cum = jnp.cumsum(ai, axis=2)
        inter = jnp.einsum('bhcd,bhde->bhce', qi * jnp.exp(cum), state)
        diff = cum[:, :, :, None, :] - cum[:, :, None, :, :]
        decay = jnp.exp(jnp.where(mask[:, :, None], diff, -jnp.inf))
        scores = jnp.einsum('bhid,bhjd,bhijd->bhij', qi, ki, decay)
        intra = jnp.einsum('bhij,bhje->bhie', scores, vi)
        last = cum[:, :, -1:, :]
        new_state = jnp.exp(last[:, :, 0, :])[..., None] * state + jnp.einsum('bhcd,bhce->bhde', ki * jnp.exp(last - cum), vi)
        return new_state, inter + intra

    s0 = jnp.zeros((b, h, dk, dv), jnp.float32)
    _, o = lax.scan(step, s0, (qc, kc, vc, ac))
    return o.transpose(1, 2, 0, 3, 4).reshape(b, h, t, dv).astype(v.dtype)


def mixing_sublayer(h, w_in, sb_q_gain, sb_k_gain, dil_q_gain, dil_k_gain, w_gla_a, b_gla_a, gla_out_gain,
                    w_br_sb, w_br_dil, w_br_gla, w_out):
    b, t, d = h.shape
    proj = h @ w_in
    idx = [int(i) for i in np.cumsum(IN_SPLITS)[:-1]]
    (q_sb, k_sb, v_sb, q_dil, k_dil, v_dil, q_gla, k_gla, v_gla, r_gla, a_gla, gate_cols) = jnp.split(proj, idx, axis=-1)

    qa = rms_norm(split_heads(q_sb, SB_HEADS), sb_q_gain)
    ka = rms_norm(split_heads(k_sb, SB_HEADS), sb_k_gain)
    o_sb = merge_heads(stick_breaking_attention(qa, ka, split_heads(v_sb, SB_HEADS)))

    qb = rms_norm(split_heads(q_dil, DIL_HEADS), dil_q_gain)
    kb = rms_norm(split_heads(k_dil, DIL_HEADS), dil_k_gain)
    o_dil = merge_heads(dilated_mixture(qb, kb, split_heads(v_dil, DIL_HEADS), alibi_slopes(DIL_HEADS)))

    a = (a_gla @ w_gla_a + b_gla_a).astype(jnp.float32)
    log_a = jax.nn.log_sigmoid(a) / GLA_GATE_TAU
    o_g = gla_chunked(split_heads(q_gla, GLA_HEADS), split_heads(k_gla, GLA_HEADS),
                      split_heads(v_gla, GLA_HEADS), split_heads(log_a, GLA_HEADS))
    o_gla = merge_heads(rms_norm(o_g, gla_out_gain)) * jax.nn.silu(r_gla)

    g = jax.nn.sigmoid(gate_cols.reshape(b, t, N_BRANCH, d))
    y = (g[:, :, 0] * (o_sb @ w_br_sb)
         + g[:, :, 1] * (o_dil @ w_br_dil)
         + g[:, :, 2] * (o_gla @ w_br_gla))
    return y @ w_out


def swiglu(h, w_ffn_in, w_ffn_out):
    gate, up = jnp.split(h @ w_ffn_in, 2, axis=-1)
    return (jax.nn.silu(gate) * up) @ w_ffn_out


def setup_inputs(seed: int = 0) -> dict:
    key = jax.random.key(seed)
    ks = jax.random.split(key, 20)
    L, D = DEPTH, D_MODEL

    def normal(k, shape, scale):
        return jax.random.normal(k, shape, jnp.float32) * scale

    return {
        "x": normal(ks[0], (BATCH, SEQ, D), 1.0),
        "c": normal(ks[1], (BATCH, D), 1.0),
        "w_ada": normal(ks[2], (L, D, N_MOD * D), 0.5 * D ** -0.5),
        "b_ada": normal(ks[3], (L, N_MOD * D), 0.02),
        "norm1_gain": 1.0 + normal(ks[4], (L, D), 0.02),
        "norm2_gain": 1.0 + normal(ks[5], (L, D), 0.02),
        "w_in": normal(ks[6], (L, D, IN_WIDTH), D ** -0.5),
        "sb_q_gain": 1.0 + normal(ks[7], (L, HEAD_DIM), 0.02),
        "sb_k_gain": 1.0 + normal(ks[8], (L, HEAD_DIM), 0.02),
        "dil_q_gain": 1.0 + normal(ks[9], (L, HEAD_DIM), 0.02),
        "dil_k_gain": 1.0 + normal(ks[10], (L, HEAD_DIM), 0.02),
        "w_gla_a": normal(ks[11], (L, GLA_GATE_RANK, GLA_QK_W), GLA_GATE_RANK ** -0.5),
        "b_gla_a": normal(ks[12], (L, GLA_QK_W), 0.02),
        "gla_out_gain": 1.0 + normal(ks[13], (L, GLA_DV), 0.02),
        "w_br_sb": normal(ks[14], (L, SB_W, D), SB_W ** -0.5),
        "w_br_dil": normal(ks[15], (L, DIL_OUT_W, D), DIL_OUT_W ** -0.5),
        "w_br_gla": normal(ks[16], (L, GLA_V_W, D), GLA_V_W ** -0.5),
        "w_out": normal(ks[17], (L, D, D), D ** -0.5),
        "w_ffn_in": normal(ks[18], (L, D, 2 * D_FF), D ** -0.5),
        "w_ffn_out": normal(ks[19], (L, D_FF, D), D_FF ** -0.5),
    }


def reference(x, c, w_ada, b_ada, norm1_gain, norm2_gain, w_in, sb_q_gain, sb_k_gain, dil_q_gain, dil_k_gain,
              w_gla_a, b_gla_a, gla_out_gain, w_br_sb, w_br_dil, w_br_gla, w_out, w_ffn_in, w_ffn_out):
    for l in range(DEPTH):
        mod = jax.nn.silu(c) @ w_ada[l] + b_ada[l]
        shift1, scale1, gate1, shift2, scale2, gate2 = jnp.split(mod[:, None, :], N_MOD, axis=-1)
        h = rms_norm(x, norm1_gain[l]) * (1 + scale1) + shift1
        x = x + gate1 * mixing_sublayer(h, w_in[l], sb_q_gain[l], sb_k_gain[l], dil_q_gain[l], dil_k_gain[l],
                                        w_gla_a[l], b_gla_a[l], gla_out_gain[l],
                                        w_br_sb[l], w_br_dil[l], w_br_gla[l], w_out[l])
        h = rms_norm(x, norm2_gain[l]) * (1 + scale2) + shift2
        x = x + gate2 * swiglu(h, w_ffn_in[l], w_ffn_out[l])
    return x
```

```python
import numpy as np
import concourse.bass as bass
import concourse.mybir as mybir
from concourse.bass_utils import run_bass_kernel_spmd

F32 = mybir.dt.float32
BF16 = mybir.dt.bfloat16
ALU = mybir.AluOpType
AF = mybir.ActivationFunctionType

NCORES = 8
T = 8192
D = 2048
TL = T // NCORES
NDC = 1
NSH = NCORES // NDC
KC = D // 128
DFF = 5632
EPS = 1e-6
ENGS = ("pe", "act", "dve", "pool", "sp")


class Buf:
    def __init__(self, name, t, kind):
        self.name = name
        self.t = t
        self.kind = kind
        self.wtok = []
        self.rtok = []
        self.dsem = None
        self.dma_n = 0

    def __getitem__(self, idx):
        return self.t[idx]


class K:
    def __init__(self, nc):
        self.nc = nc
        self.stack = []
        self.sems = {}
        self.prog = {e: [] for e in ENGS}
        self.cnt = {e: 0 for e in ENGS}
        self.seen = {e: {} for e in ENGS}
        self.nsem = 0
        for e in ENGS:
            self._sem("E_" + e)
        self.out_tokens = []
        self.nbuf = 0

    def _enter(self, cm):
        v = cm.__enter__()
        self.stack.append(cm)
        return v

    def close(self):
        while self.stack:
            self.stack.pop().__exit__(None, None, None)

    def _sem(self, key):
        if key not in self.sems:
            self.sems[key] = self._enter(self.nc.semaphore("s%d" % self.nsem))
            self.nsem += 1
        return self.sems[key]

    def sb(self, name, shape, dt):
        return Buf(name, self._enter(self.nc.sbuf_tensor(name, list(shape), dt)), "sb")

    def ps(self, name, shape, dt=F32):
        return Buf(name, self._enter(self.nc.psum_tensor(name, list(shape), dt)), "ps")

    def dram(self, name, shape, dt, kind="Internal"):
        t = self.nc.dram_tensor(name, list(shape), dt, kind=kind)
        return Buf(name, t.ap(), "dram")

    def _waits(self, eng, toks):
        need = {}
        for (k, v) in toks:
            if v > need.get(k, 0):
                need[k] = v
        out = []
        for k, v in need.items():
            if self.seen[eng].get(k, 0) < v:
                self.seen[eng][k] = v
                out.append((k, v))
        return out

    def op(self, eng, fn, reads=(), writes=()):
        toks = []
        for b in reads:
            toks += b.wtok
        for b in writes:
            toks += b.wtok
            toks += b.rtok
        if eng == "pe":
            toks = [t for t in toks if t[0] != "E_pe"]
        waits = self._waits(eng, toks)
        self.cnt[eng] += 1
        tok = ("E_" + eng, self.cnt[eng])
        self.prog[eng].append(("op", waits, fn, tok))
        for b in reads:
            b.rtok = [t for t in b.rtok if t[0] != tok[0]] + [tok]
        for b in writes:
            b.wtok = [tok]
            b.rtok = []
        return tok

    def dma(self, q, out_b, out_ap, in_b, in_ap, final=False):
        return self.dma_group(q, out_b, in_b, [(out_ap, in_ap)], final=final)

    def dma_group(self, q, out_b, in_b, pairs, final=False):
        sbb = out_b if out_b.kind != "dram" else in_b
        if sbb.dsem is None:
            sbb.dsem = "D_" + sbb.name
            self._sem(sbb.dsem)
        toks = list(in_b.wtok)
        if out_b.kind != "dram":
            toks += list(out_b.wtok) + list(out_b.rtok)
        waits = self._waits(q, toks)
        sbb.dma_n += len(pairs)
        tok = (sbb.dsem, 16 * sbb.dma_n)
        for i, pr in enumerate(pairs):
            self.prog[q].append(("dma", waits if i == 0 else [], pr, tok))
        in_b.rtok = [t for t in in_b.rtok if t[0] != tok[0]] + [tok]
        if out_b.kind == "dram":
            out_b.wtok = [t for t in out_b.wtok if t[0] != tok[0]] + [tok]
        else:
            out_b.wtok = [tok]
        out_b.rtok = []
        if final:
            self.out_tokens.append(tok)
        return tok

    def mm(self, ob, o, lb, l, rb, r, start=True, stop=True):
        return self.op("pe", lambda e: e.matmul(o, lhsT=l, rhs=r, start=start, stop=stop),
                       reads=[lb, rb], writes=[ob])

    def act(self, ob, o, ib, i, func, bias=None, scale=None, extra=()):
        kw = {}
        if bias is not None:
            kw["bias"] = bias
        if scale is not None:
            kw["scale"] = scale
        return self.op("act", lambda e: e.activation(out=o, in_=i, func=func, **kw),
                       reads=[ib, *extra], writes=[ob])

    def tt(self, eng, ob, o, ab, a, bb, b, op):
        return self.op(eng, lambda e: e.tensor_tensor(out=o, in0=a, in1=b, op=op),
                       reads=[ab, bb], writes=[ob])

    def ts(self, eng, ob, o, ab, a, s1, s2, op0, op1=None, extra=()):
        if op1 is None:
            return self.op(eng, lambda e: e.tensor_scalar(out=o, in0=a, scalar1=s1, scalar2=None, op0=op0),
                           reads=[ab, *extra], writes=[ob])
        return self.op(eng, lambda e: e.tensor_scalar(out=o, in0=a, scalar1=s1, scalar2=s2, op0=op0, op1=op1),
                       reads=[ab, *extra], writes=[ob])

    def stt(self, ob, o, ab, a, scalar, bb, b, op0, op1, extra=()):
        return self.op("dve", lambda e: e.scalar_tensor_tensor(out=o, in0=a, scalar=scalar, in1=b, op0=op0, op1=op1),
                       reads=[ab, bb, *extra], writes=[ob])

    def copy(self, eng, ob, o, ib, i):
        if eng == "act":
            return self.op("act", lambda e: e.copy(out=o, in_=i), reads=[ib], writes=[ob])
        return self.op(eng, lambda e: e.tensor_copy(out=o, in_=i), reads=[ib], writes=[ob])

    def recip(self, ob, o, ib, i):
        return self.op("dve", lambda e: e.reciprocal(out=o, in_=i), reads=[ib], writes=[ob])

    def memset(self, eng, ob, o, val):
        return self.op(eng, lambda e: e.memset(o, val), reads=[], writes=[ob])

    def finish(self):
        nc = self.nc
        fin = self._waits("sp", self.out_tokens)
        self.prog["sp"].append(("wait", fin, None, None))
        sems = self.sems
        prog = self.prog
        waited = {}
        for e in ENGS:
            for kind, waits, fn, tok in prog[e]:
                for (kk, v) in waits:
                    if kk.startswith("E_"):
                        waited.setdefault(kk, set()).add(v)
        rank = {kk: {v: i + 1 for i, v in enumerate(sorted(vs))} for kk, vs in waited.items()}

        def wv(kk, v):
            return rank[kk][v] if kk.startswith("E_") else v

        def run(e, lst):
            for kind, waits, fn, tok in lst:
                for (kk, v) in waits:
                    e.wait_ge(sems[kk], wv(kk, v))
                if kind == "op":
                    ins = fn(e)
                    if tok[1] in waited.get(tok[0], ()):
                        ins.then_inc(sems[tok[0]], 1)
                elif kind == "dma":
                    o, i = fn
                    e.dma_start(out=o, in_=i).then_inc(sems[tok[0]], 16)

        with nc.Block() as block:
            @block.tensor
            def _(e):
                run(e, prog["pe"])

            @block.scalar
            def _(e):
                run(e, prog["act"])

            @block.vector
            def _(e):
                run(e, prog["dve"])

            @block.gpsimd
            def _(e):
                run(e, prog["pool"])

            @block.sync
            def _(e):
                run(e, prog["sp"])
        self.close()


class Rot:
    def __init__(self, bufs):
        self.bufs = bufs
        self.i = 0

    def next(self):
        b = self.bufs[self.i % len(self.bufs)]
        self.i += 1
        return b


class Carver:
    def __init__(self, k, name, ncols, dt):
        self.big = k.sb(name, [128, ncols], dt)
        self.cur = 0
        self.live = []
        self.prev = []

    def new_section(self):
        need = {}
        for b in self.live:
            for (kk, v) in list(b.wtok) + list(b.rtok):
                if v > need.get(kk, 0):
                    need[kk] = v
        for (kk, v) in self.prev:
            if v > need.get(kk, 0):
                need[kk] = v
        self.prev = list(need.items())
        self.live = []
        self.cur = 0

    def get(self, name, ncols):
        b = Buf(name, self.big.t[:, self.cur:self.cur + ncols], "sb")
        b.rtok = list(self.prev)
        self.cur += ncols
        self.live.append(b)
        return b


USE_SWDGE = False


class WLoader:
    def __init__(self, k, ncols):
        self.k = k
        self.ncols = ncols
        if not USE_SWDGE:
            self.stage = Rot([k.sb("wst%d" % i, [128, ncols], F32) for i in range(4)])
        self.i = 0

    def load(self, wb, dst_aps, src_buf, src_aps):
        k = self.k
        if USE_SWDGE:
            k.dma_group("pool", wb, src_buf, list(zip(dst_aps, src_aps)))
            return
        for d, s_ in zip(dst_aps, src_aps):
            st = self.stage.next()
            n = d.shape[-1]
            k.dma("sp" if self.i % 2 == 0 else "act", st, st[:, 0:n], src_buf, s_)
            k.copy("pool", wb, d, st, st[:, 0:n])
            self.i += 1


def kc_layout(W):
    Kd, N = W.shape
    return np.ascontiguousarray(W.reshape(Kd // 128, 128, N).transpose(1, 0, 2))


def pk_layout(v):
    return np.ascontiguousarray(v.reshape(-1, 128).T)


NMB = 24


def build_M():
    nc = bass.Bass("TRN2", target_bir_lowering=False)
    k = K(nc)
    cpk = k.dram("cpk", [128, KC], F32, kind="ExternalInput")
    wada = k.dram("wada", [128, KC, NMB * 128], F32, kind="ExternalInput")
    bada = k.dram("bada", [128, NMB], F32, kind="ExternalInput")
    modc = k.dram("modc", [128, NMB], F32, kind="ExternalOutput")
    c_sb = k.sb("c_sb", [128, KC], F32)
    sc = k.sb("sc", [128, KC], F32)
    b_sb = k.sb("b_sb", [128, NMB], F32)
    o_sb = k.sb("o_sb", [128, NMB], F32)
    wbs = Rot([k.sb("wm%d" % i, [128, KC, 512], F32) for i in range(2)])
    P = k.ps("pm", [128, 512], F32)
    k.dma("sp", c_sb, c_sb[:], cpk, cpk[:])
    k.dma("sp", b_sb, b_sb[:], bada, bada[:])
    k.act(sc, sc[:], c_sb, c_sb[:], AF.Silu)
    for g in range(NMB // 4):
        wb = wbs.next()
        k.dma("sp" if g % 2 == 0 else "act", wb, wb[:], wada, wada[:, :, g * 512:(g + 1) * 512])
        for b in range(4):
            j = g * 4 + b
            for kc in range(KC):
                k.mm(P, P[:, j:j + 1], wb, wb[:, kc, b * 128:(b + 1) * 128], sc, sc[:, kc:kc + 1],
                     start=(kc == 0), stop=(kc == KC - 1))
    k.tt("dve", o_sb, o_sb[:], P, P[:, 0:NMB], b_sb, b_sb[:], ALU.add)
    k.dma("sp", modc, modc[:], o_sb, o_sb[:], final=True)
    k.finish()
    return nc


def run_M(inp):
    nc = build_M()
    w_ada = inp["w_ada"]
    b_ada = inp["b_ada"]
    cpk = pk_layout(inp["c"][0])
    maps = []
    for c in range(NCORES):
        wl, bl = [], []
        for j in range(NMB):
            g = c * NMB + j
            l, m, fc = g // 96, (g % 96) // 16, g % 16
            col = m * D + fc * 128
            wl.append(kc_layout(w_ada[l][:, col:col + 128]))
            bl.append(b_ada[l][col:col + 128])
        maps.append({"cpk": cpk, "wada": np.ascontiguousarray(np.concatenate(wl, axis=2)),
                     "bada": np.ascontiguousarray(np.stack(bl, axis=1))})
    res = run_bass_kernel_spmd(nc, maps, core_ids=list(range(NCORES)))
    modall = np.concatenate([r["modc"] for r in res.results], axis=1)
    return modall


def setup_common(k):
    cm = {}
    ones = k.sb("ones", [128, 128], F32)
    k.memset("pool", ones, ones[:], 1.0)
    cm["ones"] = ones
    cm["ps"] = Rot([k.ps("pb%d" % i, [128, 512], F32) for i in range(8)])
    return cm


def norm_mod(k, cm, xd, xview, hT, gain_ap_buf, gain_ap, scale_ap, shift_ap, modb, scratch):
    ones = cm["ones"]
    xcs = scratch["xc"]
    sqs = scratch["sq"]
    rs = scratch["rs"]
    AB = scratch["AB"]
    SS = [cm["ps"].next(), cm["ps"].next()]
    for kc in range(KC):
        xc = xcs.next()
        k.dma("sp", xc, xc[:], xd, xview[:, kc, :])
        sq = sqs.next()
        k.act(sq, sq[:], xc, xc[:], AF.Square)
        for tl in range(2):
            k.mm(SS[tl], SS[tl][:], ones, ones[:], sq, sq[:, tl * 512:(tl + 1) * 512],
                 start=(kc == 0), stop=(kc == KC - 1))
    for tl in range(2):
        sl = slice(tl * 512, (tl + 1) * 512)
        k.ts("dve", rs, rs[:, sl], SS[tl], SS[tl][:], 1.0 / D, EPS, ALU.mult, ALU.add)
    k.act(rs, rs[:], rs, rs[:], AF.Sqrt)
    k.recip(rs, rs[:], rs, rs[:])
    k.ts("dve", AB, AB[:, 0:KC], modb, scale_ap, 1.0, None, ALU.add)
    k.tt("dve", AB, AB[:, 0:KC], AB, AB[:, 0:KC], gain_ap_buf, gain_ap, ALU.mult)
    for kc in range(KC):
        xc = xcs.next()
        k.dma("sp", xc, xc[:], xd, xview[:, kc, :])
        sq = sqs.next()
        k.stt(sq, sq[:], xc, xc[:], AB[:, kc:kc + 1], rs, rs[:], ALU.mult, ALU.mult, extra=[AB])
        k.act(hT, hT[:, kc, :], sq, sq[:], AF.Identity, bias=shift_ap[:, kc:kc + 1], extra=[modb])


NFM = 88
FM_COLS = NFM * 128
TM_COLS = 3072
GF_COLS = 32 * TL
LOC_COLS = 56 * TL


def build_A():
    nc = bass.Bass("TRN2", target_bir_lowering=False)
    k = K(nc)
    xT = k.dram("xT", [NSH, D, TL], F32, kind="ExternalInput")
    cpk = k.dram("cpk", [128, KC], F32, kind="ExternalInput")
    wada = k.dram("wada", [128, KC, 96 * 128], F32, kind="ExternalInput")
    bada = k.dram("bada", [128, 96], F32, kind="ExternalInput")
    modo = k.dram("modo", [128, 96], F32, kind="ExternalOutput")
    n1g = k.dram("n1g", [128, KC], F32, kind="ExternalInput")
    qkg = k.dram("qkg", [128, 4], F32, kind="ExternalInput")
    wfm = k.dram("wfm", [128, KC, FM_COLS], F32, kind="ExternalInput")
    wtm = k.dram("wtm", [128, KC, TM_COLS], F32, kind="ExternalInput")
    wag = k.dram("wag", [128, KC, 16], F32, kind="ExternalInput")
    wga = k.dram("wga", [16, 512], F32, kind="ExternalInput")
    bga = k.dram("bga", [128, 512], F32, kind="ExternalInput")
    GFa = k.dram("GF", [NSH, 128, GF_COLS], BF16, kind="ExternalOutput")
    LOCa = k.dram("LOC", [NSH, 128, LOC_COLS], BF16, kind="ExternalOutput")
    GTa = k.dram("GT", [NSH, TL, TM_COLS], BF16, kind="ExternalOutput")
    GLa = k.dram("GL", [NSH, TL, 512], F32, kind="ExternalOutput")
    cm = setup_common(k)
    ones = cm["ones"]
    PS = cm["ps"]

    hT = k.sb("hT", [128, KC, TL], BF16)
    modb = k.sb("modb", [128, 96], F32)
    gnb = k.sb("gnb", [128, KC], F32)
    qkb = k.sb("qkb", [128, 4], F32)
    scratch = {
        "xc": Rot([k.sb("xc%d" % i, [128, TL], F32) for i in range(2)]),
        "sq": Rot([k.sb("sq%d" % i, [128, TL], F32) for i in range(2)]),
        "rs": k.sb("rs", [128, TL], F32),
        "AB": k.sb("AB", [128, 2 * KC], F32),
    }
    c_sb = k.sb("c_sb", [128, KC], F32)
    sc = k.sb("sc", [128, KC], F32)
    b_sb = k.sb("b_sb", [128, 96], F32)
    k.dma("sp", c_sb, c_sb[:], cpk, cpk[:])
    k.dma("sp", b_sb, b_sb[:], bada, bada[:])
    k.act(sc, sc[:], c_sb, c_sb[:], AF.Silu)
    wms = Rot([k.sb("wm%d" % i, [128, KC, 512], F32) for i in range(2)])
    PM = PS.next()
    for g in range(24):
        wm = wms.next()
        k.dma_group("sp" if g % 2 == 0 else "act", wm, wada,
                    [(wm[:, kc, :], wada[:, kc, g * 512:(g + 1) * 512]) for kc in range(KC)])
        for b in range(4):
            j = g * 4 + b
            for kc in range(KC):
                k.mm(PM, PM[:, j:j + 1], wm, wm[:, kc, b * 128:(b + 1) * 128], sc, sc[:, kc:kc + 1],
                     start=(kc == 0), stop=(kc == KC - 1))
    k.tt("dve", modb, modb[:], PM, PM[:, 0:96], b_sb, b_sb[:], ALU.add)
    k.dma("sp", modo, modo[:], modb, modb[:])
    k.dma("sp", gnb, gnb[:], n1g, n1g[:])
    k.dma("sp", qkb, qkb[:], qkg, qkg[:])
    k.ts("dve", qkb, qkb[:, 0:1], qkb, qkb[:, 0:1], 128.0 ** -0.5, None, ALU.mult)
    k.ts("dve", qkb, qkb[:, 2:3], qkb, qkb[:, 2:3], 128.0 ** -0.5, None, ALU.mult)

    WB = Rot([k.sb("wb%d" % i, [128, KC, 512], BF16) for i in range(2)])
    WL = WLoader(k, 512)
    stg = Rot([k.sb("stg%d" % i, [128, 512], BF16) for i in range(4)])
    tmp = Rot([k.sb("tmp%d" % i, [128, 512], F32) for i in range(3)])
    wab = k.sb("wab", [128, KC, 16], BF16)
    wabf = k.sb("wabf", [128, KC, 16], F32)
    k.dma("sp", wabf, wabf[:], wag, wag[:])
    k.copy("pool", wab, wab[:], wabf, wabf[:])
    wgb = k.sb("wgb", [16, 512], F32)
    k.dma("sp", wgb, wgb[:], wga, wga[:])
    bgb = k.sb("bgb", [128, 512], F32)
    k.dma("sp", bgb, bgb[:], bga, bga[:])
    aT = k.sb("aT", [16, TL], F32)
    lst = Rot([k.sb("lst%d" % i, [128, 512], F32) for i in range(2)])
    for sh in range(NSH):
        xview = xT.t[sh].rearrange("(kc p) t -> p kc t", p=128)
        norm_mod(k, cm, xT, xview, hT, gnb, gnb[:], modb[:, 16:32], modb[:, 0:16], modb, scratch)


        def store(dst_buf, dst_ap, st):
            k.dma("sp", dst_buf, dst_ap, st, st[:])

        def epi_fm(fb, tl, P):
            tsl = slice(tl * 512, (tl + 1) * 512)
            st = stg.next()
            if fb < 24:
                gcol = fb // 6
                sq = tmp.next()
                k.act(sq, sq[:], P, P[:], AF.Square)
                S2 = PS.next()
                k.mm(S2, S2[:], ones, ones[:], sq, sq[:])
                r = tmp.next()
                k.ts("dve", r, r[:], S2, S2[:], 1.0 / 128, EPS, ALU.mult, ALU.add)
                k.act(r, r[:], r, r[:], AF.Sqrt)
                k.recip(r, r[:], r, r[:])
                k.stt(st, st[:], P, P[:], qkb[:, gcol:gcol + 1], r, r[:], ALU.mult, ALU.mult, extra=[qkb])
                store(GFa, GFa[sh][:, fb * TL + tl * 512: fb * TL + (tl + 1) * 512], st)
            elif fb < 32:
                k.copy("dve", st, st[:], P, P[:])
                store(GFa, GFa[sh][:, fb * TL + tl * 512: fb * TL + (tl + 1) * 512], st)
            else:
                lb = fb - 32
                k.act(st, st[:], P, P[:], AF.Silu if fb < 40 else AF.Sigmoid)
                store(LOCa, LOCa[sh][:, lb * TL + tl * 512: lb * TL + (tl + 1) * 512], st)

        for g in range(FM_COLS // 512):
            wb = WB.next()
            WL.load(wb, [wb[:, kc, :] for kc in range(KC)], wfm, [wfm[:, kc, g * 512:(g + 1) * 512] for kc in range(KC)])
            for blk in range(4):
                Ps = [PS.next(), PS.next()]
                for kc in range(KC):
                    for tl in range(2):
                        k.mm(Ps[tl], Ps[tl][:], wb, wb[:, kc, blk * 128:(blk + 1) * 128],
                             hT, hT[:, kc, tl * 512:(tl + 1) * 512], start=(kc == 0), stop=(kc == KC - 1))
                for tl in range(2):
                    epi_fm(g * 4 + blk, tl, Ps[tl])

        for g in range(TM_COLS // 512):
            wb = WB.next()
            WL.load(wb, [wb[:, kc, :] for kc in range(KC)], wtm, [wtm[:, kc, g * 512:(g + 1) * 512] for kc in range(KC)])
            for tb in range(TL // 128):
                P = PS.next()
                for kc in range(KC):
                    k.mm(P, P[:], hT, hT[:, kc, tb * 128:(tb + 1) * 128], wb, wb[:, kc, :],
                         start=(kc == 0), stop=(kc == KC - 1))
                st = stg.next()
                if tb % 2 == 0:
                    k.copy("dve", st, st[:], P, P[:])
                else:
                    k.copy("act", st, st[:], P, P[:])
                store(GTa, GTa[sh][tb * 128:(tb + 1) * 128, g * 512:(g + 1) * 512], st)

        for tl in range(2):
            P = PS.next()
            for kc in range(KC):
                k.mm(P, P[0:16, :], wab, wab[:, kc, :], hT, hT[:, kc, tl * 512:(tl + 1) * 512],
                     start=(kc == 0), stop=(kc == KC - 1))
            k.copy("dve", aT, aT[:, tl * 512:(tl + 1) * 512], P, P[0:16, :])
        for tb in range(TL // 128):
            P = PS.next()
            k.mm(P, P[:], aT, aT[:, tb * 128:(tb + 1) * 128], wgb, wgb[:])
            a = tmp.next()
            k.tt("dve", a, a[:], P, P[:], bgb, bgb[:], ALU.add)
            k.act(a, a[:], a, a[:], AF.Exp, scale=-1.0)
            k.act(a, a[:], a, a[:], AF.Ln, bias=1.0)
            ls = lst.next()
            k.ts("dve", ls, ls[:], a, a[:], -1.0 / 16.0, None, ALU.mult)
            k.dma("sp", GLa, GLa[sh][tb * 128:(tb + 1) * 128, :], ls, ls[:])


    for b in (GFa, LOCa, GTa, GLa, modo):
        k.out_tokens += b.wtok
    k.finish()
    return nc


def a_weights(inp, l):
    w = inp["w_in"][l]
    o = np.cumsum((0, 768, 768, 768, 768, 768, 768, 512, 512, 1024, 1024, 16, 6144))
    sl = lambda i: w[:, o[i]:o[i + 1]]
    q_sb, k_sb, v_sb, q_dil, k_dil, v_dil, q_gla, k_gla, v_gla, r_gla, a_gla, gates = [sl(i) for i in range(12)]
    wfm = kc_layout(np.concatenate([q_sb, k_sb, q_dil, k_dil, q_gla, k_gla, r_gla, gates], axis=1))
    wtm = kc_layout(np.concatenate([v_sb, v_dil, v_gla, k_gla], axis=1))
    wag = kc_layout(np.ascontiguousarray(a_gla))
    qkg = np.ascontiguousarray(np.stack([inp["sb_q_gain"][l], inp["sb_k_gain"][l],
                                         inp["dil_q_gain"][l], inp["dil_k_gain"][l]], axis=1))
    wa = inp["w_ada"][l]
    wada = np.ascontiguousarray(np.concatenate(
        [kc_layout(wa[:, m * D + fc * 128: m * D + (fc + 1) * 128]) for m in range(6) for fc in range(16)], axis=2))
    bada = np.ascontiguousarray(np.stack(
        [inp["b_ada"][l][m * D + fc * 128: m * D + (fc + 1) * 128] for m in range(6) for fc in range(16)], axis=1))
    return {"wfm": wfm, "wtm": wtm, "wag": wag, "qkg": qkg, "wada": wada, "bada": bada, "cpk": pk_layout(inp["c"][0]),
            "n1g": pk_layout(inp["norm1_gain"][l]),
            "wga": np.ascontiguousarray(inp["w_gla_a"][l]),
            "bga": np.ascontiguousarray(np.broadcast_to(inp["b_gla_a"][l][None, :], (128, 512)))}


def run_A(ncA, inp, l, xT_full):
    ws = a_weights(inp, l)
    maps = []
    for c in range(NDC):
        m = dict(ws)
        m["xT"] = np.ascontiguousarray(np.stack([xT_full[:, (c * NSH + sh) * TL:(c * NSH + sh + 1) * TL] for sh in range(NSH)]))
        maps.append(m)
    res = run_bass_kernel_spmd(ncA, maps, core_ids=list(range(NDC))).results
    ranks = [{kk: res[r // NSH][kk][r % NSH] for kk in ("GF", "LOC", "GT", "GL")} for r in range(NCORES)]
    return ranks, np.ascontiguousarray(res[0]["modo"])


O_COLS = 16384


def build_C():
    nc = bass.Bass("TRN2", target_bir_lowering=False)
    k = K(nc)
    OG = k.dram("OG", [NSH, 128, 16 * TL], BF16, kind="ExternalInput")
    LOC = k.dram("LOC", [NSH, 128, LOC_COLS], BF16, kind="ExternalInput")
    xT = k.dram("xT", [NSH, D, TL], F32, kind="ExternalInput")
    modv = k.dram("modv", [128, 96], F32, kind="ExternalInput")
    gog = k.dram("gog", [128, 2], F32, kind="ExternalInput")
    n2g = k.dram("n2g", [128, KC], F32, kind="ExternalInput")
    wbr = k.dram("wbr", [128, KC, D], F32, kind="ExternalInput")
    wout = k.dram("wout", [128, KC, D], F32, kind="ExternalInput")
    wffi = k.dram("wffi", [128, KC, 2 * DFF], F32, kind="ExternalInput")
    wffo = k.dram("wffo", [128, DFF // 128, D], F32, kind="ExternalInput")
    rid = k.dram("rid", [1, 8], F32, kind="ExternalInput")
    xo = k.dram("xo", [NSH, D, TL], F32, kind="ExternalOutput")
    x1d = k.dram("x1d", [NSH, D, TL], F32, kind="Internal")
    return nc, k, dict(OG=OG, LOC=LOC, xT=xT, modv=modv, gog=gog, n2g=n2g, wbr=wbr, wout=wout,
                       wffi=wffi, wffo=wffo, xo=xo, x1d=x1d)


def emit_C(k, d):
    OG, LOC, xT, modv, gog, n2g = d["OG"], d["LOC"], d["xT"], d["modv"], d["gog"], d["n2g"]
    wbr, wout, wffi, wffo, xo, x1d = d["wbr"], d["wout"], d["wffi"], d["wffo"], d["xo"], d["x1d"]
    cm = setup_common(k)
    ones = cm["ones"]
    PS = cm["ps"]
    NF = DFF // 128
    HT = k.sb("HT", [128, KC, TL], BF16)
    BIG = k.sb("BIG", [128, NF, TL], BF16)
    modb = k.sb("modb", [128, 96], F32)
    gnb = k.sb("gnb", [128, KC], F32)
    gob = k.sb("gob", [128, 2], F32)
    scratch = {
        "xc": Rot([k.sb("xc%d" % i, [128, TL], F32) for i in range(2)]),
        "sq": Rot([k.sb("sq%d" % i, [128, TL], F32) for i in range(2)]),
        "rs": k.sb("rs", [128, TL], F32),
        "AB": k.sb("AB", [128, 2 * KC], F32),
    }
    WB = Rot([k.sb("wb%d" % i, [128, 8192], BF16) for i in range(2)])
    tmp = Rot([k.sb("tmp%d" % i, [128, 512], F32) for i in range(4)])
    srs = Rot([k.sb("sr%d" % i, [128, TL], BF16) for i in range(2)])
    gts = Rot([k.sb("gt%d" % i, [128, 3, TL], BF16) for i in range(2)])
    k.dma("sp", modb, modb[:], modv, modv[:])
    k.dma("sp", gnb, gnb[:], n2g, n2g[:])
    k.dma("sp", gob, gob[:], gog, gog[:])

    WL = WLoader(k, 512)

    def wload(wb, src, ncol, c0, nk):
        WL.load(wb, [wb[:, kc * ncol:(kc + 1) * ncol] for kc in range(nk)], src, [src[:, kc, c0:c0 + ncol] for kc in range(nk)])

    for sh in range(NSH):
        k.dma_group("sp", BIG, OG, [(BIG[:, j, :], OG[sh][:, j * TL:(j + 1) * TL]) for j in range(16)])
        for hd in range(4):
            rr = scratch["rs"]
            for tl in range(2):
                tsl = slice(tl * 512, (tl + 1) * 512)
                S2 = PS.next()
                for half in range(2):
                    sq = tmp.next()
                    k.act(sq, sq[:], BIG, BIG[:, 8 + hd * 2 + half, tsl], AF.Square)
                    k.mm(S2, S2[:], ones, ones[:], sq, sq[:], start=(half == 0), stop=(half == 1))
                k.ts("dve", rr, rr[:, tsl], S2, S2[:], 1.0 / 256, EPS, ALU.mult, ALU.add)
            k.act(rr, rr[:], rr, rr[:], AF.Sqrt)
            k.recip(rr, rr[:], rr, rr[:])
            for half in range(2):
                b = hd * 2 + half
                sr = srs.next()
                k.dma("sp", sr, sr[:], LOC, LOC[sh][:, b * TL:(b + 1) * TL])
                t = scratch["sq"].next()
                k.stt(t, t[:], BIG, BIG[:, 8 + b, :], gob[:, half:half + 1], rr, rr[:], ALU.mult, ALU.mult, extra=[gob])
                k.tt("dve", BIG, BIG[:, 8 + b, :], t, t[:], sr, sr[:], ALU.mult)

        KR = ((0, 6), (6, 8), (8, 16))
        for g in range(4):
            wb = WB.next()
            wload(wb, wbr, 512, g * 512, KC)
            for blk in range(4):
                fo = g * 4 + blk
                gt = gts.next()
                k.dma_group("sp", gt, LOC, [(gt[:, b, :], LOC[sh][:, (8 + b * 16 + fo) * TL:(9 + b * 16 + fo) * TL]) for b in range(3)])
                for tl in range(2):
                    tsl = slice(tl * 512, (tl + 1) * 512)
                    Pb = [PS.next() for _ in range(3)]
                    for b, (k0, k1) in enumerate(KR):
                        for kc in range(k0, k1):
                            k.mm(Pb[b], Pb[b][:], wb, wb[:, kc * 512 + blk * 128: kc * 512 + (blk + 1) * 128],
                                 BIG, BIG[:, kc, tsl], start=(kc == k0), stop=(kc == k1 - 1))
                    ta, tb_, tc = tmp.next(), tmp.next(), tmp.next()
                    k.tt("dve", ta, ta[:], Pb[0], Pb[0][:], gt, gt[:, 0, tsl], ALU.mult)
                    k.tt("dve", tb_, tb_[:], Pb[1], Pb[1][:], gt, gt[:, 1, tsl], ALU.mult)
                    k.tt("dve", tc, tc[:], Pb[2], Pb[2][:], gt, gt[:, 2, tsl], ALU.mult)
                    k.tt("pool", ta, ta[:], ta, ta[:], tb_, tb_[:], ALU.add)
                    k.tt("pool", HT, HT[:, fo, tsl], ta, ta[:], tc, tc[:], ALU.add)

        xview = xT.t[sh].rearrange("(kc p) t -> p kc t", p=128)
        x1view = x1d.t[sh].rearrange("(kc p) t -> p kc t", p=128)
        xoview = xo.t[sh].rearrange("(kc p) t -> p kc t", p=128)
        for g in range(4):
            wb = WB.next()
            wload(wb, wout, 512, g * 512, KC)
            for blk in range(4):
                fo = g * 4 + blk
                xc = scratch["xc"].next()
                k.dma("sp", xc, xc[:], xT, xview[:, fo, :])
                x1c = scratch["sq"].next()
                for tl in range(2):
                    tsl = slice(tl * 512, (tl + 1) * 512)
                    P = PS.next()
                    for kc in range(KC):
                        k.mm(P, P[:], wb, wb[:, kc * 512 + blk * 128: kc * 512 + (blk + 1) * 128],
                             HT, HT[:, kc, tsl], start=(kc == 0), stop=(kc == KC - 1))
                    k.stt(x1c, x1c[:, tsl], P, P[:], modb[:, 32 + fo:33 + fo], xc, xc[:, tsl], ALU.mult, ALU.add, extra=[modb])
                k.dma("sp", x1d, x1view[:, fo, :], x1c, x1c[:])

        norm_mod(k, cm, x1d, x1view, HT, gnb, gnb[:], modb[:, 64:80], modb[:, 48:64], modb, scratch)

        for g in range(NF // 2):
            wb = WB.next()
            wload(wb, wffi, 512, g * 512, KC)
            for jj in range(2):
                for tl in range(2):
                    tsl = slice(tl * 512, (tl + 1) * 512)
                    Pg, Pu = PS.next(), PS.next()
                    for (P, blk) in ((Pg, jj), (Pu, 2 + jj)):
                        for kc in range(KC):
                            k.mm(P, P[:], wb, wb[:, kc * 512 + blk * 128: kc * 512 + (blk + 1) * 128],
                                 HT, HT[:, kc, tsl], start=(kc == 0), stop=(kc == KC - 1))
                    s = tmp.next()
                    k.act(s, s[:], Pg, Pg[:], AF.Silu)
                    k.tt("dve", BIG, BIG[:, 2 * g + jj, tsl], Pu, Pu[:], s, s[:], ALU.mult)

        for g in range(KC):
            wb = WB.next()
            wload(wb, wffo, 128, g * 128, NF)
            fo = g
            xc = scratch["xc"].next()
            k.dma("sp", xc, xc[:], x1d, x1view[:, fo, :])
            oc = scratch["sq"].next()
            for tl in range(2):
                tsl = slice(tl * 512, (tl + 1) * 512)
                P = PS.next()
                for kc in range(NF):
                    k.mm(P, P[:], wb, wb[:, kc * 128:(kc + 1) * 128], BIG, BIG[:, kc, tsl],
                         start=(kc == 0), stop=(kc == NF - 1))
                k.stt(oc, oc[:, tsl], P, P[:], modb[:, 80 + fo:81 + fo], xc, xc[:, tsl], ALU.mult, ALU.add, extra=[modb])
            k.dma("sp", xo, xoview[:, fo, :], oc, oc[:])

    k.out_tokens += xo.wtok
    k.finish()


def c_weights(inp, l):
    wbr = kc_layout(np.concatenate([inp["w_br_sb"][l], inp["w_br_dil"][l], inp["w_br_gla"][l]], axis=0))
    wfi = inp["w_ffn_in"][l]
    cols = []
    for g in range(DFF // 256):
        cols.append(wfi[:, g * 256:(g + 1) * 256])
        cols.append(wfi[:, DFF + g * 256:DFF + (g + 1) * 256])
    return {"wbr": wbr, "wout": kc_layout(inp["w_out"][l]),
            "wffi": kc_layout(np.concatenate(cols, axis=1)),
            "wffo": kc_layout(inp["w_ffn_out"][l]),
            "n2g": pk_layout(inp["norm2_gain"][l]),
            "gog": pk_layout(inp["gla_out_gain"][l]),
            "rid": np.zeros((1, 8), np.float32)}


SB_STEPS = (32, 64)
NST = sum(SB_STEPS)
DIL_R = (1, 4, 16)


def build_B():
    nc = bass.Bass("TRN2", target_bir_lowering=False)
    k = K(nc)
    sbq = k.dram("sbq", [128, 6 * 2 * 512], BF16, kind="ExternalInput")
    sbk = k.dram("sbk", [128, 6 * NST * 128], BF16, kind="ExternalInput")
    sbv = k.dram("sbv", [128, 6 * NST * 128], BF16, kind="ExternalInput")
    cst = k.dram("cst", [128, 4 * 512 + 5 * 128], F32, kind="ExternalInput")
    dq = k.dram("dq", [128, 3 * 2048], BF16, kind="ExternalInput")
    dk = k.dram("dk", [128, 3 * 32 * 128], BF16, kind="ExternalInput")
    dv = k.dram("dv", [128, 3 * 32 * 128], BF16, kind="ExternalInput")
    dbias = k.dram("dbias", [128, 3 * 2 * 256], F32, kind="ExternalInput")
    gq = k.dram("gq", [128, T], BF16, kind="ExternalInput")
    gk = k.dram("gk", [128, T], BF16, kind="ExternalInput")
    gkt = k.dram("gkt", [128, T], BF16, kind="ExternalInput")
    gv = k.dram("gv", [128, T], BF16, kind="ExternalInput")
    gla = k.dram("gla", [128, T], F32, kind="ExternalInput")
    OB = k.dram("OB", [128, O_COLS], BF16, kind="ExternalOutput")

    PS = Rot([k.ps("pb%d" % i, [128, 512], F32) for i in range(6)])
    PO = Rot([k.ps("po%d" % i, [128, 512], F32) for i in range(2)])
    cs = k.sb("cst_sb", [128, 4 * 512 + 5 * 128], F32)
    k.dma("sp", cs, cs[:], cst, cst[:])
    C0 = 4 * 512
    negtri, negones = cs[:, C0:C0 + 128], cs[:, C0 + 128:C0 + 256]
    triblk, onesblk, upstrict = cs[:, C0 + 256:C0 + 384], cs[:, C0 + 384:C0 + 512], cs[:, C0 + 512:C0 + 640]
    mbf = k.sb("mbf", [128, 4 * 512], BF16)
    k.copy("dve", mbf, mbf[:], cs, cs[:, 0:C0])
    onesb = k.sb("onesb", [128, 128], BF16)
    k.memset("pool", onesb, onesb[:], 1.0)
    stg = Rot([k.sb("stg%d" % i, [128, 512], BF16) for i in range(2)])
    c16 = Carver(k, "big16", 42240, BF16)
    c32 = Carver(k, "big32", 9472, F32)

    qsb = c16.get("qsb", 6 * 2 * 512)
    k.dma("sp", qsb, qsb[:], sbq, sbq[:])
    Kb = Rot([c16.get("Kb%d" % i, 64 * 128) for i in range(2)])
    Vb = Rot([c16.get("Vb%d" % i, 64 * 128) for i in range(2)])
    ebuf = Rot([c32.get("eb%d" % i, 512) for i in range(2)])
    spb = Rot([c32.get("sp%d" % i, 512) for i in range(3)])
    accb = Rot([c32.get("acc%d" % i, 512) for i in range(2)])
    wbuf = Rot([c16.get("w%d" % i, 512) for i in range(3)])
    for h in range(6):
        s0 = 0
        for slot in range(2):
            nst = SB_STEPS[slot]
            kb_, vb_ = Kb.next(), Vb.next()
            base = (h * NST + s0) * 128
            k.dma("sp", kb_, kb_[:, 0:nst * 128], sbk, sbk[:, base:base + nst * 128])
            k.dma("act", vb_, vb_[:, 0:nst * 128], sbv, sbv[:, base:base + nst * 128])
            qt = qsb[:, (h * 2 + slot) * 512:(h * 2 + slot + 1) * 512]
            O = PO.next()
            acc = None
            for s in range(nst):
                kblk = kb_[:, s * 128:(s + 1) * 128]
                vblk = vb_[:, s * 128:(s + 1) * 128]
                A = PS.next()
                k.mm(A, A[:], kb_, kblk, qsb, qt)
                e = ebuf.next()
                k.act(e, e[:], A, A[:], AF.Exp)
                sp = spb.next()
                k.act(sp, sp[:], e, e[:], AF.Ln, bias=1.0)
                if s < 4:
                    k.tt("pool", sp, sp[:], sp, sp[:], cs, cs[:, (3 - s) * 512:(4 - s) * 512], ALU.mult)
                B = PS.next()
                k.mm(B, B[:], kb_, kblk, qsb, qt, start=True, stop=False)
                k.mm(B, B[:], cs, negtri, sp, sp[:], start=False, stop=(acc is None))
                if acc is not None:
                    k.mm(B, B[:], cs, negones, acc, acc[:], start=False, stop=True)
                w = wbuf.next()
                k.act(w, w[:], B, B[:], AF.Exp)
                if s < 4:
                    k.tt("pool", w, w[:], w, w[:], mbf, mbf[:, (3 - s) * 512:(4 - s) * 512], ALU.mult)
                k.mm(O, O[:], vb_, vblk, w, w[:], start=(s == 0), stop=(s == nst - 1))
                if s < nst - 1:
                    nacc = accb.next()
                    if acc is None:
                        k.copy("pool", nacc, nacc[:], sp, sp[:])
                    else:
                        k.tt("pool", nacc, nacc[:], acc, acc[:], sp, sp[:], ALU.add)
                    acc = nacc
            st = stg.next()
            k.copy("dve", st, st[:], O, O[:])
            k.dma("sp", OB, OB[:, (h * 2 + slot) * 512:(h * 2 + slot + 1) * 512], st, st[:])
            s0 += nst

    c16.new_section()
    c32.new_section()
    dqb = c16.get("dqb", 3 * 2048)
    dkb = c16.get("dkb", 3 * 32 * 128)
    dvb = c16.get("dvb", 3 * 32 * 128)
    dbb = c32.get("dbb", 3 * 2 * 256)
    k.dma("sp", dqb, dqb[:], dq, dq[:])
    k.dma("sp", dkb, dkb[:], dk, dk[:])
    k.dma("act", dvb, dvb[:], dv, dv[:])
    k.dma("sp", dbb, dbb[:], dbias, dbias[:])
    k.act(dbb, dbb[:], dbb, dbb[:], AF.Exp)
    accO = c32.get("accO", 2048)
    accD = c32.get("accD", 2048)
    pbuf = Rot([c32.get("pf%d" % i, 256) for i in range(2)])
    pbb = Rot([c16.get("pb16_%d" % i, 256) for i in range(2)])
    for g, r in enumerate(DIL_R):
        nblk = 16 // r
        for rho in range(r):
            for n in range(nblk):
                qi = rho * nblk + n
                ki = rho * (nblk + 1) + n
                qv = dqb[:, g * 2048 + qi * 128: g * 2048 + (qi + 1) * 128]
                kp = dkb[:, (g * 32 + ki) * 128:(g * 32 + ki + 1) * 128]
                kc_ = dkb[:, (g * 32 + ki + 1) * 128:(g * 32 + ki + 2) * 128]
                vp = dvb[:, (g * 32 + ki) * 128:(g * 32 + ki + 1) * 128]
                vc = dvb[:, (g * 32 + ki + 1) * 128:(g * 32 + ki + 2) * 128]
                S = PS.next()
                k.mm(S, S[:, 0:128], dkb, kp, dqb, qv)
                k.mm(S, S[:, 128:256], dkb, kc_, dqb, qv)
                p = pbuf.next()
                k.act(p, p[:], S, S[:, 0:256], AF.Exp)
                pb = pbb.next()
                var = 1 if n == 0 else 0
                k.tt("dve", pb, pb[:], p, p[:], dbb, dbb[:, (g * 2 + var) * 256:(g * 2 + var + 1) * 256], ALU.mult)
                Pq = PS.next()
                k.mm(Pq, Pq[:, 0:128], dvb, vp, pb, pb[:, 0:128], start=True, stop=False)
                k.mm(Pq, Pq[:, 0:128], dvb, vc, pb, pb[:, 128:256], start=False, stop=True)
                k.mm(Pq, Pq[:, 128:256], onesb, onesb[:], pb, pb[:, 0:128], start=True, stop=False)
                k.mm(Pq, Pq[:, 128:256], onesb, onesb[:], pb, pb[:, 128:256], start=False, stop=True)
                lo = rho + r * 128 * n
                hi = lo + r * 127 + 1
                ao = accO[:, lo:hi:r] if r > 1 else accO[:, lo:hi]
                ad = accD[:, lo:hi:r] if r > 1 else accD[:, lo:hi]
                if g == 0:
                    k.copy("dve", accO, ao, Pq, Pq[:, 0:128])
                    k.copy("dve", accD, ad, Pq, Pq[:, 128:256])
                else:
                    k.tt("dve", accO, ao, Pq, Pq[:, 0:128], accO, ao, ALU.add)
                    k.tt("dve", accD, ad, Pq, Pq[:, 128:256], accD, ad, ALU.add)
    k.recip(accD, accD[:], accD, accD[:])
    dout = c16.get("dout", 2048)
    k.tt("dve", dout, dout[:], accO, accO[:], accD, accD[:], ALU.mult)
    k.dma("sp", OB, OB[:, 6144:8192], dout, dout[:])

    c16.new_section()
    c32.new_section()
    gqb = c16.get("gqb", T)
    gkb = c16.get("gkb", T)
    gktb = c16.get("gktb", T)
    gvb = c16.get("gvb", T)
    glb = c32.get("glb", T)
    k.dma("sp", gqb, gqb[:], gq, gq[:])
    k.dma("sp", gkb, gkb[:], gk, gk[:])
    k.dma("act", gktb, gktb[:], gkt, gkt[:])
    k.dma("act", gvb, gvb[:], gv, gv[:])
    k.dma("sp", glb, glb[:], gla, gla[:])
    ogb = c16.get("ogb", T)
    Sf = Rot([c32.get("Sf%d" % i, 128) for i in range(2)])
    Sb = Rot([c16.get("Sb%d" % i, 128) for i in range(2)])
    ex1 = Rot([c32.get("ex1_%d" % i, 256) for i in range(2)])
    ex2 = Rot([c32.get("ex2_%d" % i, 128) for i in range(2)])
    ex3 = Rot([c32.get("ex3_%d" % i, 128) for i in range(2)])
    qtb = Rot([c16.get("qt%d" % i, 128) for i in range(2)])
    ktb = Rot([c16.get("kt%d" % i, 128) for i in range(2)])
    khb = Rot([c16.get("kh%d" % i, 128) for i in range(2)])
    scb = Rot([c16.get("sc%d" % i, 128) for i in range(2)])
    sf = Sf.next()
    k.memset("pool", sf, sf[:], 0.0)
    sbf = Sb.next()
    k.memset("pool", sbf, sbf[:], 0.0)
    for b in range(T // 128):
        bs = slice(b * 128, (b + 1) * 128)
        C1 = PS.next()
        k.mm(C1, C1[:, 0:128], glb, glb[:, bs], cs, triblk)
        k.mm(C1, C1[:, 128:256], glb, glb[:, bs], cs, onesblk)
        k.mm(C1, C1[:, 256:384], cs, upstrict, glb, glb[:, bs])
        e1, e2, e3 = ex1.next(), ex2.next(), ex3.next()
        k.act(e1, e1[:], C1, C1[:, 0:256], AF.Exp)
        k.act(e2, e2[:], C1, C1[:, 0:128], AF.Exp, scale=-1.0)
        k.act(e3, e3[:], C1, C1[:, 256:384], AF.Exp)
        qt_, kt_, kh_ = qtb.next(), ktb.next(), khb.next()
        k.stt(qt_, qt_[:], gqb, gqb[:, bs], 128.0 ** -0.5, e1, e1[:, 0:128], ALU.mult, ALU.mult)
        k.tt("dve", kt_, kt_[:], gkb, gkb[:, bs], e2, e2[:], ALU.mult)
        k.tt("pool", kh_, kh_[:], gktb, gktb[:, bs], e3, e3[:], ALU.mult)
        SC = PS.next()
        k.mm(SC, SC[:, 0:128], kt_, kt_[:], qt_, qt_[:])
        sc_ = scb.next()
        k.tt("dve", sc_, sc_[:], SC, SC[:, 0:128], cs, triblk, ALU.mult)
        OP = PS.next()
        for ch in range(2):
            c0, c1 = 64 * ch, 64 * ch + 64
            k.mm(OP, OP[:, c0:c1], sbf, sbf[:], qt_, qt_[:, c0:c1], start=True, stop=False)
            k.mm(OP, OP[:, c0:c1], gvb, gvb[:, bs], sc_, sc_[:, c0:c1], start=False, stop=True)
            U = PS.next()
            k.mm(U, U[:, 0:128], kh_, kh_[c0:c1, :], gvb, gvb[c0:c1, bs])
            nsf = Sf.next()
            k.stt(nsf, nsf[:], sf, sf[:], e1[:, 128 + c0:129 + c0], U, U[:, 0:128], ALU.mult, ALU.add, extra=[e1])
            sf = nsf
            sbf = Sb.next()
            k.copy("act", sbf, sbf[:], sf, sf[:])
        k.copy("dve", ogb, ogb[:, bs], OP, OP[:, 0:128])
    k.dma("sp", OB, OB[:, 8192:16384], ogb, ogb[:])
    k.out_tokens += OB.wtok
    k.finish()
    return nc


def b_consts():
    s = np.arange(128)[:, None]
    dmask = np.concatenate([((r * 128 + s) < np.arange(512)[None, :]).astype(np.float32) for r in range(4)], axis=1)
    j = np.arange(128)[:, None]
    i = np.arange(128)[None, :]
    same = (j // 64) == (i // 64)
    negtri = -(j >= i).astype(np.float32)
    negones = -np.ones((128, 128), np.float32)
    triblk = (same & (j <= i)).astype(np.float32)
    onesblk = same.astype(np.float32)
    upstrict = (same & (j > i)).astype(np.float32)
    return np.ascontiguousarray(np.concatenate([dmask, negtri, negones, triblk, onesblk, upstrict], axis=1))


def dil_bias(hh, first_invalid):
    out = np.zeros((128, 3, 2, 256), np.float32)
    j = np.arange(128)[:, None].astype(np.float64)
    i = np.arange(128)[None, :].astype(np.float64)
    NEG = -30000.0
    for g, r in enumerate(DIL_R):
        slope = 2.0 ** (-8.0 * (2 * g + hh + 1) / 6)
        dprev = 128 + i - j
        dcur = i - j
        bprev = np.where((dprev >= 0) & (dprev <= 128), -slope * r * dprev, NEG)
        bcur = np.where((dcur >= 0) & (dcur <= 128), -slope * r * dcur, NEG)
        out[:, g, 0, 0:128] = bprev
        out[:, g, 0, 128:256] = bcur
        out[:, g, 1, 0:128] = NEG if first_invalid else bprev
        out[:, g, 1, 128:256] = bcur
    return np.ascontiguousarray(out.reshape(128, -1))


def gather_A(resA):
    GF = np.stack([r["GF"].reshape(128, 32, TL) for r in resA])
    GF = np.ascontiguousarray(GF.transpose(2, 1, 0, 3)).reshape(32, 128, T)
    GT = np.concatenate([r["GT"] for r in resA], axis=0)
    GL = np.concatenate([r["GL"] for r in resA], axis=0)
    return {"qsb": GF[0:6], "ksb": GF[6:12], "qdil": GF[12:18], "kdil": GF[18:24], "qg": GF[24:28], "kg": GF[28:32],
            "vsb": GT[:, 0:768], "vdil": GT[:, 768:1536], "vg": GT[:, 1536:2560], "kgt": GT[:, 2560:3072], "la": GL}


def tokblk(a):
    n = a.shape[0] // 128
    return np.ascontiguousarray(a.reshape(n, 128, 128).transpose(1, 0, 2)).reshape(128, n * 128)


def b_inputs(g, c, cst):
    bf = g["qsb"].dtype
    zq = np.zeros((128, 128), bf)
    sbq, sbk, sbv = [], [], []
    for h in range(6):
        for slot in range(2):
            Q = c if slot == 0 else 15 - c
            sbq.append(g["qsb"][h][:, Q * 512:(Q + 1) * 512])
            for s in range(SB_STEPS[slot]):
                kb = 4 * Q + 3 - s
                if kb >= 0:
                    sbk.append(g["ksb"][h][:, kb * 128:(kb + 1) * 128])
                    sbv.append(g["vsb"][kb * 128:(kb + 1) * 128, h * 128:(h + 1) * 128])
                else:
                    sbk.append(zq)
                    sbv.append(zq)
    hh, R = c // 4, c % 4
    dq, dk, dv = [], [], []
    ar = np.arange(128)
    for gi, r in enumerate(DIL_R):
        head = 2 * gi + hh
        nblk = 16 // r
        kblocks, vblocks = [], []
        for rho in range(r):
            for n in range(nblk):
                tok = 2048 * R + rho + r * (128 * n + ar)
                dq.append(g["qdil"][head][:, tok])
            for nb in range(-1, nblk):
                tok = 2048 * R + rho + r * (128 * nb + ar)
                if tok[0] < 0:
                    kblocks.append(zq)
                    vblocks.append(zq)
                else:
                    kblocks.append(g["kdil"][head][:, tok])
                    vblocks.append(g["vdil"][tok, head * 128:(head + 1) * 128])
        while len(kblocks) < 32:
            kblocks.append(zq)
            vblocks.append(zq)
        dk += kblocks
        dv += vblocks
    hd, half = c // 2, c % 2
    cat = lambda l: np.ascontiguousarray(np.concatenate(l, axis=1))
    return {"sbq": cat(sbq), "sbk": cat(sbk), "sbv": cat(sbv), "cst": cst,
            "dq": cat(dq), "dk": cat(dk), "dv": cat(dv), "dbias": dil_bias(hh, R == 0),
            "gq": np.ascontiguousarray(g["qg"][hd]), "gk": np.ascontiguousarray(g["kg"][hd]),
            "gkt": tokblk(g["kgt"][:, hd * 128:(hd + 1) * 128]),
            "gv": tokblk(g["vg"][:, hd * 256 + half * 128: hd * 256 + (half + 1) * 128]),
            "gla": tokblk(g["la"][:, hd * 128:(hd + 1) * 128])}


def c_og(OBs, r):
    blocks = []
    for h in range(6):
        parts = []
        for tl in range(2):
            Q = 2 * r + tl
            c, slot = (Q, 0) if Q < 8 else (15 - Q, 1)
            parts.append(OBs[c][:, (h * 2 + slot) * 512:(h * 2 + slot + 1) * 512])
        blocks.append(np.concatenate(parts, axis=1))
    R, off = r // 2, (r % 2) * 1024
    for hh in range(2):
        blocks.append(OBs[hh * 4 + R][:, 6144 + off:6144 + off + 1024])
    for c in range(8):
        blocks.append(OBs[c][:, 8192 + r * TL:8192 + (r + 1) * TL])
    return np.ascontiguousarray(np.concatenate(blocks, axis=1))


def kernel(**inp):
    import sys, time
    t0 = time.time()

    def log(msg):
        print("[kernel] %7.1fs %s" % (time.time() - t0, msg), file=sys.stderr, flush=True)

    inp = {k_: np.asarray(v) for k_, v in inp.items()}
    ncA = build_A()
    ncB = build_B()
    ncC, kC, dC = build_C()
    emit_C(kC, dC)
    cst = b_consts()
    xT = np.ascontiguousarray(inp["x"][0].T)
    cores = list(range(NCORES))
    for l in range(2):
        resA, modv = run_A(ncA, inp, l, xT)
        log("A%d done" % l)
        g = gather_A(resA)
        resB = run_bass_kernel_spmd(ncB, [b_inputs(g, c, cst) for c in cores], core_ids=cores).results
        log("B%d done" % l)
        OBs = [r["OB"] for r in resB]
        ws = c_weights(inp, l)
        maps = []
        for c in range(NDC):
            m = dict(ws)
            rs = [c * NSH + sh for sh in range(NSH)]
            m["OG"] = np.ascontiguousarray(np.stack([c_og(OBs, r) for r in rs]))
            m["LOC"] = np.ascontiguousarray(np.stack([resA[r]["LOC"] for r in rs]))
            m["xT"] = np.ascontiguousarray(np.stack([xT[:, r * TL:(r + 1) * TL] for r in rs]))
            m["modv"] = modv
            maps.append(m)
        resC = run_bass_kernel_spmd(ncC, maps, core_ids=list(range(NDC))).results
        log("C%d done" % l)
        xT = np.concatenate([resC[r // NSH]["xo"][r % NSH] for r in cores], axis=1)
    return np.ascontiguousarray(xT.T)[None].astype(np.float32)
```
